# Optimizing a Trainium2 kernel written in Bass

```python
import math
import jax, jax.numpy as jnp
from jax import lax
import numpy as np

D_MODEL = 2048
BATCH = 8
SEQ = 2048
DEPTH = 1

MLSTM_WIDTH = D_MODEL // 2
MLSTM_HEADS = 4
MLSTM_V_DIM = MLSTM_WIDTH // MLSTM_HEADS
MLSTM_QK_DIM = MLSTM_V_DIM // 2
MLSTM_CHUNK = 64
CONV_WIDTH = 4
S5_WIDTH = D_MODEL - MLSTM_WIDTH
S5_GROUP = 16
S5_GROUPS = S5_WIDTH // S5_GROUP
S5_STATE = 64
DT_MIN = 1e-3
DT_MAX = 1e-1
D_MIX = MLSTM_WIDTH + S5_WIDTH
QK_COLS = 2 * MLSTM_HEADS * MLSTM_QK_DIM
V_COLS = MLSTM_HEADS * MLSTM_V_DIM
O_COLS = MLSTM_WIDTH
GATE_COLS = MLSTM_HEADS
IN_COLS = QK_COLS + V_COLS + O_COLS + 2 * GATE_COLS + S5_WIDTH
D_FF = ((8 * D_MODEL // 3 + 255) // 256) * 256
EPS = 1e-6

kernel_name = "macaron_mlstm_s5_hybrid"


def rms_norm(x, g):
    xf = x.astype(jnp.float32)
    y = xf * lax.rsqrt(jnp.mean(xf * xf, axis=-1, keepdims=True) + EPS)
    return (y * g.astype(jnp.float32)).astype(x.dtype)


def swiglu(x, w1, w3, w2):
    return (jax.nn.silu(x @ w1) * (x @ w3)) @ w2


def causal_depthwise_conv(x, w, b):
    k = w.shape[0]
    c = x.shape[-1]
    y = lax.conv_general_dilated(x, w[:, None, :], window_strides=(1,), padding=[(k - 1, 0)],
                                 dimension_numbers=('NWC', 'WIO', 'NWC'), feature_group_count=c)
    return y + b


def mlstm_chunkwise(q, k, v, i_pre, f_pre):
    bsz, seq, nh, dk = q.shape
    dv = v.shape[-1]
    nc = seq // MLSTM_CHUNK
    L = MLSTM_CHUNK

    def to_chunks(a):
        return a.reshape(bsz, nc, L, nh, a.shape[-1]).transpose(0, 3, 1, 2, 4)

    q = to_chunks(q) * (dk ** -0.5)
    k = to_chunks(k)
    v = to_chunks(v)
    li = i_pre.reshape(bsz, nc, L, nh).transpose(0, 3, 1, 2)
    lf = jax.nn.log_sigmoid(f_pre).reshape(bsz, nc, L, nh).transpose(0, 3, 1, 2)
    b = jnp.cumsum(lf, axis=-1)
    b_end = b[..., -1]

    w_end = b_end[..., None] - b + li
    m_loc = jnp.max(w_end, axis=-1)
    e_end = jnp.exp(w_end - m_loc[..., None])
    c_loc = jnp.einsum('bhcs,bhcsv,bhcsk->bhcvk', e_end, v, k)
    n_loc = jnp.einsum('bhcs,bhcsk->bhck', e_end, k)

    def step(carry, inp):
        c_st, n_st, m_st = carry
        c_l, n_l, m_l, bl = inp
        m_new = jnp.maximum(bl + m_st, m_l)
        a = jnp.exp(bl + m_st - m_new)
        g = jnp.exp(m_l - m_new)
        c_new = a[..., None, None] * c_st + g[..., None, None] * c_l
        n_new = a[..., None] * n_st + g[..., None] * n_l
        return (c_new, n_new, m_new), (c_st, n_st, m_st)

    init = (jnp.zeros((bsz, nh, dv, dk), jnp.float32),
            jnp.zeros((bsz, nh, dk), jnp.float32),
            jnp.zeros((bsz, nh), jnp.float32))
    xs = (jnp.moveaxis(c_loc, 2, 0), jnp.moveaxis(n_loc, 2, 0),
          jnp.moveaxis(m_loc, 2, 0), jnp.moveaxis(b_end, 2, 0))
    _, (c_prev, n_prev, m_prev) = lax.scan(step, init, xs)
    c_prev = jnp.moveaxis(c_prev, 0, 2)
    n_prev = jnp.moveaxis(n_prev, 0, 2)
    m_prev = jnp.moveaxis(m_prev, 0, 2)

    causal = jnp.tril(jnp.ones((L, L), dtype=bool))
    log_d = jnp.where(causal, b[..., :, None] - b[..., None, :] + li[..., None, :], -jnp.inf)
    inter_log = b + m_prev[..., None]
    m_t = jnp.maximum(inter_log, jnp.max(log_d, axis=-1))
    d_mat = jnp.exp(log_d - m_t[..., None])
    a_t = jnp.exp(inter_log - m_t)

    s = jnp.einsum('bhctk,bhcsk->bhcts', q, k) * d_mat
    num = jnp.einsum('bhcts,bhcsv->bhctv', s, v) + a_t[..., None] * jnp.einsum('bhctk,bhcvk->bhctv', q, c_prev)
    den = jnp.sum(s, axis=-1) + a_t * jnp.einsum('bhctk,bhck->bhct', q, n_prev)
    h = num / jnp.maximum(jnp.abs(den), jnp.exp(-m_t))[..., None]
    return h.transpose(0, 2, 3, 1, 4).reshape(bsz, seq, nh, dv)


def s5_branch(u, lam_re, lam_im, log_dt, b_re, b_im, c_re, c_im, d_skip, w_glu, b_glu):
    bsz, seq, _ = u.shape
    f32 = jnp.float32
    uf = u.astype(f32).reshape(bsz, seq, S5_GROUPS, S5_GROUP)
    lam = lax.complex(lam_re.astype(f32), lam_im.astype(f32))
    dt = jnp.exp(log_dt.astype(f32))[:, None]
    lam_bar = jnp.exp(lam * dt)
    b_c = lax.complex(b_re.astype(f32), b_im.astype(f32))
    b_bar = ((lam_bar - 1.0) / lam)[..., None] * b_c
    bu = jnp.einsum('bsgh,gph->bsgp', uf.astype(jnp.complex64), b_bar)
    a = jnp.broadcast_to(lam_bar[None, None], (1, seq, S5_GROUPS, S5_STATE))

    def combine(e1, e2):
        a1, x1 = e1
        a2, x2 = e2
        return a1 * a2, a2 * x1 + x2

    _, states = lax.associative_scan(combine, (a, bu), axis=1)
    c_c = lax.complex(c_re.astype(f32), c_im.astype(f32))
    y = jnp.real(jnp.einsum('bsgp,ghp->bsgh', states, c_c)) + d_skip.astype(f32) * uf
    y = jax.nn.gelu(y).reshape(bsz, seq, S5_WIDTH)
    y = y * jax.nn.sigmoid(y @ w_glu.astype(f32) + b_glu.astype(f32))
    return y.astype(u.dtype)


def hybrid_mixer(h, w_in, conv_w, conv_b, b_i, b_f, mlstm_norm, lam_re, lam_im, log_dt,
                 b_re, b_im, c_re, c_im, d_skip, w_glu, b_glu, w_out):
    bsz, seq, _ = h.shape
    proj = h @ w_in
    idx = [QK_COLS, QK_COLS + V_COLS, QK_COLS + V_COLS + O_COLS,
           QK_COLS + V_COLS + O_COLS + GATE_COLS, QK_COLS + V_COLS + O_COLS + 2 * GATE_COLS]
    qk, v, o, ig, fg, u = jnp.split(proj, idx, axis=-1)

    qk = jax.nn.silu(causal_depthwise_conv(qk, conv_w, conv_b))
    q, k = jnp.split(qk, 2, axis=-1)
    f32 = jnp.float32
    q = q.reshape(bsz, seq, MLSTM_HEADS, MLSTM_QK_DIM).astype(f32)
    k = k.reshape(bsz, seq, MLSTM_HEADS, MLSTM_QK_DIM).astype(f32)
    vh = v.reshape(bsz, seq, MLSTM_HEADS, MLSTM_V_DIM).astype(f32)
    i_pre = ig.astype(f32) + b_i.astype(f32)
    f_pre = fg.astype(f32) + b_f.astype(f32)
    hm = mlstm_chunkwise(q, k, vh, i_pre, f_pre)
    hm = rms_norm(hm, mlstm_norm.reshape(MLSTM_HEADS, MLSTM_V_DIM))
    hm = (hm.reshape(bsz, seq, MLSTM_WIDTH) * jax.nn.sigmoid(o.astype(f32))).astype(h.dtype)

    hs = s5_branch(u, lam_re, lam_im, log_dt, b_re, b_im, c_re, c_im, d_skip, w_glu, b_glu)

    return jnp.concatenate([hm, hs], axis=-1) @ w_out


def setup_inputs(seed: int = 0) -> dict:
    key = jax.random.key(seed)
    ks = jax.random.split(key, 32)
    f32 = jnp.float32
    L = DEPTH

    def nrm(k, shape, scale):
        return jax.random.normal(k, shape, f32) * scale

    def gain(k, shape):
        return 1.0 + 0.02 * jax.random.normal(k, shape, f32)

    lam_im = math.pi * jnp.arange(S5_STATE, dtype=f32)[None, None, :] + nrm(ks[13], (L, S5_GROUPS, S5_STATE), 0.01)
    b_f = jnp.linspace(3.0, 6.0, MLSTM_HEADS, dtype=f32)[None, :] + nrm(ks[10], (L, MLSTM_HEADS), 0.1)
    return {
        "x": nrm(ks[0], (BATCH, SEQ, D_MODEL), 1.0),
        "ffn1_norm": gain(ks[1], (L, D_MODEL)),
        "ffn1_w1": nrm(ks[2], (L, D_MODEL, D_FF), D_MODEL ** -0.5),
        "ffn1_w3": nrm(ks[3], (L, D_MODEL, D_FF), D_MODEL ** -0.5),
        "ffn1_w2": nrm(ks[4], (L, D_FF, D_MODEL), D_FF ** -0.5),
        "mix_norm": gain(ks[5], (L, D_MODEL)),
        "w_in": nrm(ks[6], (L, D_MODEL, IN_COLS), D_MODEL ** -0.5),
        "conv_w": nrm(ks[7], (L, CONV_WIDTH, QK_COLS), CONV_WIDTH ** -0.5),
        "conv_b": nrm(ks[8], (L, QK_COLS), 0.01),
        "b_i": nrm(ks[9], (L, MLSTM_HEADS), 0.1),
        "b_f": b_f,
        "mlstm_norm": gain(ks[11], (L, MLSTM_WIDTH)),
        "lam_re": -0.5 + nrm(ks[12], (L, S5_GROUPS, S5_STATE), 0.01),
        "lam_im": lam_im,
        "log_dt": jax.random.uniform(ks[14], (L, S5_GROUPS), f32, math.log(DT_MIN), math.log(DT_MAX)),
        "b_re": nrm(ks[15], (L, S5_GROUPS, S5_STATE, S5_GROUP), (2 * S5_GROUP) ** -0.5),
        "b_im": nrm(ks[16], (L, S5_GROUPS, S5_STATE, S5_GROUP), (2 * S5_GROUP) ** -0.5),
        "c_re": nrm(ks[17], (L, S5_GROUPS, S5_GROUP, S5_STATE), (2 * S5_STATE) ** -0.5),
        "c_im": nrm(ks[18], (L, S5_GROUPS, S5_GROUP, S5_STATE), (2 * S5_STATE) ** -0.5),
        "d_skip": nrm(ks[19], (L, S5_GROUPS, S5_GROUP), 1.0),
        "w_glu": nrm(ks[20], (L, S5_WIDTH, S5_WIDTH), S5_WIDTH ** -0.5),
        "b_glu": nrm(ks[21], (L, S5_WIDTH), 0.01),
        "w_out": nrm(ks[22], (L, D_MIX, D_MODEL), D_MIX ** -0.5),
        "ffn2_norm": gain(ks[23], (L, D_MODEL)),
        "ffn2_w1": nrm(ks[24], (L, D_MODEL, D_FF), D_MODEL ** -0.5),
        "ffn2_w3": nrm(ks[25], (L, D_MODEL, D_FF), D_MODEL ** -0.5),
        "ffn2_w2": nrm(ks[26], (L, D_FF, D_MODEL), D_FF ** -0.5),
        "final_norm": gain(ks[27], (D_MODEL,)),
    }


def reference(x, ffn1_norm, ffn1_w1, ffn1_w3, ffn1_w2, mix_norm, w_in, conv_w, conv_b, b_i, b_f,
              mlstm_norm, lam_re, lam_im, log_dt, b_re, b_im, c_re, c_im, d_skip, w_glu, b_glu,
              w_out, ffn2_norm, ffn2_w1, ffn2_w3, ffn2_w2, final_norm):
    for l in range(DEPTH):
        x = x + 0.5 * swiglu(rms_norm(x, ffn1_norm[l]), ffn1_w1[l], ffn1_w3[l], ffn1_w2[l])
        h = rms_norm(x, mix_norm[l])
        x = x + hybrid_mixer(h, w_in[l], conv_w[l], conv_b[l], b_i[l], b_f[l], mlstm_norm[l],
                             lam_re[l], lam_im[l], log_dt[l], b_re[l], b_im[l], c_re[l], c_im[l],
                             d_skip[l], w_glu[l], b_glu[l], w_out[l])
        x = x + 0.5 * swiglu(rms_norm(x, ffn2_norm[l]), ffn2_w1[l], ffn2_w3[l], ffn2_w2[l])
    return rms_norm(x, final_norm)
```

```python
import math
from contextlib import ExitStack
import numpy as np
import concourse.bass as bass
import concourse.mybir as mybir
from concourse.bass_utils import run_bass_kernel_spmd

F32 = mybir.dt.float32
BF16 = mybir.dt.bfloat16
I32 = mybir.dt.int32
ALU = mybir.AluOpType
AF = mybir.ActivationFunctionType

D = 2048
S = 2048
DFF = 5632
NF = DFF // 128
INC = 4104
EPS = 1e-6
SEM_LIMIT = 20000
DMA_SLOTS = 12
NCONST = 800


class _Op:
    __slots__ = ("eng", "fn", "dma", "deps", "idx", "token", "has_dep", "slot_wait", "noinst")


class Prog:
    ENGS = ("pe", "act", "dve", "pool", "sp")

    def __init__(self, nc):
        self.nc = nc
        self.ops = []
        self.last_w = {}
        self.readers = {}
        self.last_on_eng = {}
        self.dma_since_bar = []

    def add(self, eng, fn, reads=(), writes=(), dma=False, noinst=False, extra_deps=()):
        op = _Op()
        op.eng = eng
        op.fn = fn
        op.dma = dma
        op.noinst = noinst
        op.idx = len(self.ops)
        op.token = None
        op.has_dep = False
        op.slot_wait = None
        deps = set(extra_deps)
        for k in reads:
            w = self.last_w.get(k)
            if w is not None:
                deps.add(w)
        for k in writes:
            w = self.last_w.get(k)
            if w is not None:
                deps.add(w)
            deps.update(self.readers.get(k, ()))
        for k in reads:
            self.readers.setdefault(k, []).append(op.idx)
        for k in writes:
            self.last_w[k] = op.idx
            self.readers[k] = []
        deps.discard(op.idx)
        op.deps = deps
        self.ops.append(op)
        if not noinst:
            if dma:
                self.dma_since_bar.append(op.idx)
            else:
                self.last_on_eng[eng] = op.idx
        return op.idx

    def barrier(self):
        deps = set(self.dma_since_bar)
        for e in ("pe", "act", "dve", "pool"):
            if e in self.last_on_eng:
                deps.add(self.last_on_eng[e])
        for e in self.ENGS:
            self.add(e, None, noinst=True, extra_deps=deps)
        self.dma_since_bar = []
        self.last_w = {}
        self.readers = {}

    def emit(self, stack):
        nc = self.nc
        ops = self.ops
        for op in ops:
            for d in op.deps:
                dop = ops[d]
                if dop.eng == "pe" and op.eng == "pe" and not dop.dma and not op.dma:
                    continue
                dop.has_dep = True
        sems = {}

        def get_sem(name):
            if name not in sems:
                sems[name] = stack.enter_context(nc.semaphore(name))
            return sems[name]

        cnt = {e: 0 for e in self.ENGS}
        dma_i = {e: 0 for e in self.ENGS}
        slot_uses = {}
        all_dma_tokens = {}
        for op in ops:
            if op.noinst:
                continue
            if op.dma:
                i = dma_i[op.eng]
                dma_i[op.eng] += 1
                name = f"d_{op.eng}_{i % DMA_SLOTS}"
                uses = slot_uses.get(name, 0)
                if uses > 0:
                    op.slot_wait = (name, 16 * uses)
                slot_uses[name] = uses + 1
                op.token = (name, 16 * (uses + 1))
                all_dma_tokens[name] = op.token
            elif op.has_dep:
                c = cnt[op.eng]
                cnt[op.eng] += 1
                op.token = (f"c_{op.eng}_{c // SEM_LIMIT}", (c % SEM_LIMIT) + 1)
        for op in ops:
            if op.token is not None:
                get_sem(op.token[0])
        per_eng = {e: [o for o in ops if o.eng == e] for e in self.ENGS}

        self.streams = {e: [] for e in self.ENGS}

        def run_engine(ename, eng):
            waited = {}
            strm = self.streams[ename]
            for op in per_eng[ename]:
                need = {}
                for d in op.deps:
                    dop = ops[d]
                    if dop.token is None:
                        continue
                    if dop.eng == "pe" and ename == "pe" and not dop.dma and not op.dma:
                        continue
                    n, v = dop.token
                    if need.get(n, 0) < v:
                        need[n] = v
                if op.slot_wait is not None:
                    n, v = op.slot_wait
                    if need.get(n, 0) < v:
                        need[n] = v
                for n in sorted(need):
                    v = need[n]
                    if waited.get(n, 0) < v:
                        eng.wait_ge(sems[n], v)
                        waited[n] = v
                        strm.append(("wait", n, v))
                if op.noinst:
                    continue
                ins = op.fn(eng)
                if op.token is not None:
                    ins.then_inc(sems[op.token[0]], 16 if op.dma else 1)
                    strm.append(("inc", op.token[0], 16 if op.dma else 1, op.idx))
                else:
                    strm.append(("inst", op.idx))
            if ename == "sp":
                for n, v in all_dma_tokens.values():
                    if waited.get(n, 0) < v:
                        eng.wait_ge(sems[n], v)
                        waited[n] = v
                        strm.append(("wait", n, v))

        block = stack.enter_context(nc.Block())

        @block.tensor
        def _(e):
            run_engine("pe", e)

        @block.scalar
        def _(e):
            run_engine("act", e)

        @block.vector
        def _(e):
            run_engine("dve", e)

        @block.gpsimd
        def _(e):
            run_engine("pool", e)

        @block.sync
        def _(e):
            run_engine("sp", e)


def simulate_streams(streams):
    val = {}
    pos = {e: 0 for e in streams}
    progress = True
    while progress:
        progress = False
        for e, st_ in streams.items():
            while pos[e] < len(st_):
                it = st_[pos[e]]
                if it[0] == "wait":
                    if val.get(it[1], 0) >= it[2]:
                        pos[e] += 1
                        progress = True
                    else:
                        break
                else:
                    if it[0] == "inc":
                        val[it[1]] = val.get(it[1], 0) + it[2]
                    pos[e] += 1
                    progress = True
    stuck = {e: (pos[e], len(st_), st_[pos[e]] if pos[e] < len(st_) else None) for e, st_ in streams.items()}
    return all(pos[e] == len(streams[e]) for e in streams), stuck, val


PARAM_SHAPES = {
    "x": [S, D], "ffn1_norm": [1, D], "ffn1_w1": [D, DFF], "ffn1_w3": [D, DFF], "ffn1_w2": [DFF, D],
    "mix_norm": [1, D], "w_in": [D, INC], "conv_w": [4, 1024], "conv_b": [1, 1024], "b_i": [1, 4], "b_f": [1, 4],
    "mlstm_norm": [1, 1024], "lam_re": [64, 64], "lam_im": [64, 64], "log_dt": [1, 64],
    "b_re": [64, 64, 16], "b_im": [64, 64, 16], "c_re": [64, 16, 64], "c_im": [64, 16, 64], "d_skip": [64, 16],
    "w_glu": [1024, 1024], "b_glu": [1, 1024], "w_out": [D, D], "ffn2_norm": [1, D], "ffn2_w1": [D, DFF],
    "ffn2_w3": [D, DFF], "ffn2_w2": [DFF, D], "final_norm": [1, D], "consts": [128, NCONST],
}
SCRATCH = {
    "xT0": ([D, S], F32), "xT1": ([D, S], F32), "xT2": ([D, S], F32), "xT3": ([D, S], F32),
    "qkT_d": ([1024, S], BF16), "uperm_d": ([1024, 8, 256], BF16), "V_d": ([S, 4, 257], BF16),
    "sigo_d": ([S, 1024], F32), "gate_d": ([S, 8], F32), "yperm_d": ([1024, 8, 256], BF16),
    "hsT_d": ([1024, S], BF16), "hmT_d": ([1024, S], BF16),
}
ALL_PHASES = ("p0", "ffn1", "proj", "s5", "glu", "mlstm", "wout", "ffn2", "final")


def make_consts():
    c = np.zeros((128, NCONST), np.float32)
    c[:, 0:128] = np.eye(128)
    c[:, 128:256] = np.triu(np.ones((128, 128)))
    c[:, 256:384] = 1.0
    r = np.arange(128) // 16
    c[:, 384:512] = (r[None, :] >= r[:, None]).astype(np.float32)
    c[:, 512:800] = np.tile(np.arange(9, dtype=np.float32), 32)[None, :]
    return c


def build_nc(phases=ALL_PHASES, ext_in=(), ext_out=()):
    nc = bass.Bass("TRN2", target_bir_lowering=False)
    T = {}
    for k, shp in PARAM_SHAPES.items():
        T[k] = nc.dram_tensor(k, shp, F32, kind="ExternalInput").ap()
    for k, (shp, dt) in SCRATCH.items():
        kind = "ExternalInput" if k in ext_in else ("ExternalOutput" if k in ext_out else "Internal")
        T[k] = nc.dram_tensor(k, shp, dt, kind=kind).ap()
    out = nc.dram_tensor("out", [S, D], F32, kind="ExternalOutput").ap()

    with ExitStack() as st:
        P = Prog(nc)

        def SB(stack, name, shape, dt=F32):
            return stack.enter_context(nc.sbuf_tensor(name, shape, dt))

        banks = [st.enter_context(nc.psum_tensor(f"bank{i}", [128, 512], F32)) for i in range(8)]
        cst = SB(st, "cst", [128, NCONST])
        ident_bf = SB(st, "ident_bf", [128, 128], BF16)
        ones_bf = SB(st, "ones_bf", [128, 128], BF16)
        gate_sb = SB(st, "gate_sb", [128, 16, 8])
        a8 = SB(st, "a8", [128, 32, 4])
        ident = cst[:, 0:128]
        tri = cst[:, 128:256]
        ones = cst[:, 256:384]
        m5 = cst[:, 384:512]
        kk = cst[:, 512:800].rearrange("p (g k) -> p g k", k=9)

        def DMA(q, out_, in_, r=(), w=(), slow=False):
            if slow:
                fn = lambda e: e.dma_start(out=out_, in_=in_, allow_slow_non_contiguous=True)
            else:
                fn = lambda e: e.dma_start(out=out_, in_=in_)
            P.add(q, fn, r, w, dma=True)

        def ACT(fn, r=(), w=()):
            P.add("act", fn, r, w)

        def DVE(fn, r=(), w=()):
            P.add("dve", fn, r, w)

        def PE(fn, r=(), w=()):
            P.add("pe", fn, r, w)

        def bcast_rows(ap2d, n):
            return bass.AP(tensor=ap2d.tensor, offset=ap2d.offset, ap=[[0, 128], [1, n]])

        DMA("sp", cst[:], T["consts"], w=["cst"])
        ACT(lambda e: e.copy(out=ident_bf[:], in_=ident), r=["cst"], w=["identbf"])
        ACT(lambda e: e.copy(out=ones_bf[:], in_=ones), r=["cst"], w=["onesbf"])
        P.barrier()

        def phase0():
            with ExitStack() as ps:
                xtm = [SB(ps, f"xtm{b}", [128, D]) for b in range(2)]
                stg = [SB(ps, f"stg{b}", [128, 16, 128]) for b in range(2)]
                dstv = T["xT0"].rearrange("(dt p) t -> p dt t", p=128)
                import os
                for i in range(int(os.environ.get('P0_N', 16))):
                    b = i % 2
                    DMA("sp", xtm[b][:], T["x"][i * 128:(i + 1) * 128, :], w=[("xtm", b)])
                    for dq in range(4):
                        k = dq

                        def f(e, b=b, dq=dq, k=k):
                            ins = None
                            for j in range(4):
                                c0 = (dq * 4 + j) * 128
                                ins = e.matmul(banks[k][:, j * 128:(j + 1) * 128], lhsT=xtm[b][:, c0:c0 + 128], rhs=ident, start=True, stop=True)
                            return ins
                        PE(f, r=[("xtm", b)], w=[("bank", k)])
                        src = banks[k][:].rearrange("p (j t) -> p j t", j=4)
                        dst = stg[b][:, dq * 4:(dq + 1) * 4, :]
                        if dq % 2 == 0:
                            ACT(lambda e, src=src, dst=dst: e.copy(out=dst, in_=src), r=[("bank", k)], w=[("stg", b, dq)])
                        else:
                            DVE(lambda e, src=src, dst=dst: e.tensor_copy(out=dst, in_=src), r=[("bank", k)], w=[("stg", b, dq)])
                    if os.environ.get("P0_STORE", "1") == "1":
                        DMA("sp", dstv[:, :, i * 128:(i + 1) * 128], stg[b][:], r=[("stg", b, dq) for dq in range(4)])
                    elif os.environ.get("P0_STORE") == "2":
                        DMA("sp", T["xT1"][i * 128:(i + 1) * 128, :], stg[b][:].rearrange("p a b -> p (a b)"), r=[("stg", b, dq) for dq in range(4)])
                P.barrier()

        def norm_fm(ps, src, t0, ntb, gcol, emit2, tag):
            xt = [SB(ps, f"nxt{tag}{b}", [128, 512]) for b in range(2)]
            sq = [SB(ps, f"nsq{tag}{b}", [128, 512], BF16) for b in range(2)]
            rstd = SB(ps, f"nrstd{tag}", [128, 512])

            def run():
                for tb in range(ntb):
                    c0 = t0 + tb * 512
                    for dt in range(16):
                        b = dt % 2
                        DMA("sp", xt[b][:], src[dt * 128:(dt + 1) * 128, c0:c0 + 512], w=[("nx", b)])
                        ACT(lambda e, b=b: e.activation(out=sq[b][:], in_=xt[b][:], func=AF.Square), r=[("nx", b)], w=[("nsq", b)])
                        PE(lambda e, b=b, dt=dt: e.matmul(banks[7][:], lhsT=ones_bf[:], rhs=sq[b][:], start=(dt == 0), stop=(dt == 15)),
                           r=[("nsq", b)], w=[("bank", 7)])
                    ACT(lambda e: e.activation(out=rstd[:], in_=banks[7][:], func=AF.Ln, scale=1.0 / D, bias=EPS), r=[("bank", 7)], w=["rstd"])
                    ACT(lambda e: e.activation(out=rstd[:], in_=rstd[:], func=AF.Exp, scale=-0.5), r=["rstd"], w=["rstd"])
                    for dt in range(16):
                        b = dt % 2
                        DMA("sp", xt[b][:], src[dt * 128:(dt + 1) * 128, c0:c0 + 512], w=[("nx", b)])
                        emit2(tb, dt, xt[b], ("nx", b), rstd, gcol)
            return run

        def load_gcol(ps, name, vec):
            g = SB(ps, name, [128, 16])
            DMA("sp", g[:], vec.rearrange("o (dt p) -> p (o dt)", p=128), w=[name], slow=True)
            return g

        def ffn(src, dst, gvec, W1, W3, W2, tag):
            with ExitStack() as ps:
                gcol = load_gcol(ps, "gcol" + tag, gvec)
                xn = SB(ps, "xn" + tag, [128, 16, 1024], BF16)
                hT = SB(ps, "hT" + tag, [128, NF, 1024], BF16)
                w1s = [SB(ps, f"w1s{tag}{b}", [128, 16, 256], BF16) for b in range(2)]
                w3s = [SB(ps, f"w3s{tag}{b}", [128, 16, 256], BF16) for b in range(2)]
                w2s = [SB(ps, f"w2s{tag}{b}", [128, NF, 128], BF16) for b in range(2)]
                sl = [SB(ps, f"sl{tag}{b}", [128, 512]) for b in range(2)]
                xr = [SB(ps, f"xr{tag}{b}", [128, 512]) for b in range(2)]
                xo = [SB(ps, f"xo{tag}{b}", [128, 512]) for b in range(2)]

                def emit2(tb, dt, xtile, xkey, rstd, g):
                    DVE(lambda e: e.scalar_tensor_tensor(out=xn[:, dt, tb * 512:(tb + 1) * 512], in0=xtile[:], scalar=g[:, dt:dt + 1],
                                                         in1=rstd[:], op0=ALU.mult, op1=ALU.mult),
                        r=[xkey, "rstd", "gcol" + tag], w=[("xn", dt, tb)])
                for hf in range(2):
                    norm_fm_run = norm_fm(ps, src, hf * 1024, 2, gcol, emit2, tag + str(hf))
                    norm_fm_run()
                    cnt = 0
                    for fp in range(NF // 2):
                        wb = fp % 2
                        DMA("pool", w1s[wb][:], W1[:, fp * 256:(fp + 1) * 256].rearrange("(kt p) n -> p kt n", p=128), w=[("w1s", wb)])
                        DMA("pool", w3s[wb][:], W3[:, fp * 256:(fp + 1) * 256].rearrange("(kt p) n -> p kt n", p=128), w=[("w3s", wb)])
                        for fi in range(2):
                            fc = fp * 2 + fi
                            for tt in range(2):
                                k = cnt % 2
                                cnt += 1
                                xkeys = [("xn", dt, tt) for dt in range(16)]

                                def mmA(e, wb=wb, fi=fi, tt=tt, k=k):
                                    ins = None
                                    for kt in range(16):
                                        ins = e.matmul(banks[k][:], lhsT=w1s[wb][:, kt, fi * 128:(fi + 1) * 128],
                                                       rhs=xn[:, kt, tt * 512:(tt + 1) * 512], start=(kt == 0), stop=(kt == 15))
                                    return ins

                                def mmB(e, wb=wb, fi=fi, tt=tt, k=k):
                                    ins = None
                                    for kt in range(16):
                                        ins = e.matmul(banks[2 + k][:], lhsT=w3s[wb][:, kt, fi * 128:(fi + 1) * 128],
                                                       rhs=xn[:, kt, tt * 512:(tt + 1) * 512], start=(kt == 0), stop=(kt == 15))
                                    return ins
                                PE(mmA, r=[("w1s", wb)] + xkeys, w=[("bank", k)])
                                PE(mmB, r=[("w3s", wb)] + xkeys, w=[("bank", 2 + k)])
                                ACT(lambda e, k=k: e.activation(out=sl[k][:], in_=banks[k][:], func=AF.Silu), r=[("bank", k)], w=[("sl", k)])
                                DVE(lambda e, k=k, fc=fc, tt=tt: e.tensor_tensor(out=hT[:, fc, tt * 512:(tt + 1) * 512], in0=sl[k][:],
                                                                                 in1=banks[2 + k][:], op=ALU.mult),
                                    r=[("sl", k), ("bank", 2 + k)], w=[("hT", fc, tt)])
                    cnt = 0
                    for dt in range(16):
                        wb = dt % 2
                        DMA("pool", w2s[wb][:], W2[:, dt * 128:(dt + 1) * 128].rearrange("(fc p) n -> p fc n", p=128), w=[("w2s", wb)])
                        for tt in range(2):
                            k = cnt % 2
                            cnt += 1
                            c0 = hf * 1024 + tt * 512
                            DMA("sp", xr[k][:], src[dt * 128:(dt + 1) * 128, c0:c0 + 512], w=[("xr", k)])

                            def mmC(e, wb=wb, tt=tt, k=k):
                                ins = None
                                for fc in range(NF):
                                    ins = e.matmul(banks[4 + k][:], lhsT=w2s[wb][:, fc, :], rhs=hT[:, fc, tt * 512:(tt + 1) * 512],
                                                   start=(fc == 0), stop=(fc == NF - 1))
                                return ins
                            PE(mmC, r=[("w2s", wb)] + [("hT", fc, tt) for fc in range(NF)], w=[("bank", 4 + k)])
                            DVE(lambda e, k=k: e.scalar_tensor_tensor(out=xo[k][:], in0=banks[4 + k][:], scalar=0.5, in1=xr[k][:],
                                                                      op0=ALU.mult, op1=ALU.add),
                                r=[("bank", 4 + k), ("xr", k)], w=[("xo", k)])
                            DMA("sp", dst[dt * 128:(dt + 1) * 128, c0:c0 + 512], xo[k][:], r=[("xo", k)])
                P.barrier()

        def proj():
            with ExitStack() as ps:
                gcol = load_gcol(ps, "gcolm", T["mix_norm"])
                hT = SB(ps, "hTm", [128, 16, S], BF16)
                slab = [SB(ps, f"slab{b}", [128, 16, 512], BF16) for b in range(2)]
                qraw = [SB(ps, f"qraw{b}", [128, 3 + S]) for b in range(2)]
                acc = SB(ps, "cacc", [128, S])
                qk_o = [SB(ps, f"qko{b}", [128, S], BF16) for b in range(2)]
                up_sb = [SB(ps, f"upsb{b}", [128, 8, 256], BF16) for b in range(2)]
                v_sb = [SB(ps, f"vsb{b}", [128, 2, 257], BF16) for b in range(2)]
                so_sb = [SB(ps, f"sosb{b}", [128, 512]) for b in range(2)]
                cw = SB(ps, "cw", [128, 8, 4])
                cb = SB(ps, "cb", [128, 8])
                for kx in range(4):
                    DMA("sp", cw[:, :, kx], T["conv_w"][kx:kx + 1, :].rearrange("o (c p) -> p (o c)", p=128), w=[("cw", kx)], slow=True)
                DMA("sp", cb[:], T["conv_b"].rearrange("o (c p) -> p (o c)", p=128), w=["cb"], slow=True)
                for b in range(2):
                    DVE(lambda e, b=b: e.memset(qraw[b][:, 0:3], 0.0), w=[("qraw", b, "pad")])
                    DVE(lambda e, b=b: e.memset(v_sb[b][:, :, 256:257], 1.0), w=[("vone", b)])

                def emit2(tb, dt, xtile, xkey, rstd, g):
                    DVE(lambda e: e.scalar_tensor_tensor(out=hT[:, dt, tb * 512:(tb + 1) * 512], in0=xtile[:], scalar=g[:, dt:dt + 1],
                                                         in1=rstd[:], op0=ALU.mult, op1=ALU.mult),
                        r=[xkey, "rstd", "gcolm"], w=[("h", dt, tb)])
                norm_fm(ps, T["xT1"], 0, 4, gcol, emit2, "m")()
                hkeys = [("h", dt, tb) for dt in range(16) for tb in range(4)]
                slabs = [("qk", 0, 512), ("qk", 512, 512), ("v", 1024, 512), ("v", 1536, 512), ("o", 2048, 512), ("o", 2560, 512),
                         ("g", 3072, 8), ("u", 3080, 512), ("u", 3592, 512)]
                bankc = [0]

                def nbank():
                    k = bankc[0] % 6
                    bankc[0] += 1
                    return k
                cnt = {"qk": 0, "v": 0, "o": 0, "u": 0, "ev": 0}
                import os
                kinds_ = os.environ.get("PROJ_KINDS", "qk,v,o,g,u").split(",")
                slabs = [s_ for s_ in slabs if s_[0] in kinds_]
                for si, (kind, c0, ncol) in enumerate(slabs):
                    sbuf = slab[si % 2]
                    skey = ("slab", si % 2)
                    DMA("pool", sbuf[:, :, 0:ncol], T["w_in"][:, c0:c0 + ncol].rearrange("(kt p) n -> p kt n", p=128), w=[skey])
                    if kind in ("qk", "u"):
                        for ci in range(4):
                            cidx = cnt[kind]
                            cnt[kind] += 1
                            b = cidx % 2
                            for tt in range(4):
                                k = nbank()

                                def mm(e, sbuf=sbuf, ci=ci, tt=tt, k=k):
                                    ins = None
                                    for kt in range(16):
                                        ins = e.matmul(banks[k][:], lhsT=sbuf[:, kt, ci * 128:(ci + 1) * 128],
                                                       rhs=hT[:, kt, tt * 512:(tt + 1) * 512], start=(kt == 0), stop=(kt == 15))
                                    return ins
                                PE(mm, r=[skey] + hkeys, w=[("bank", k)])
                                if kind == "qk":
                                    ACT(lambda e, b=b, tt=tt, k=k: e.copy(out=qraw[b][:, 3 + tt * 512:3 + (tt + 1) * 512], in_=banks[k][:]),
                                        r=[("bank", k)], w=[("qraw", b, tt)])
                                else:
                                    ACT(lambda e, b=b, tt=tt, k=k: e.copy(out=up_sb[b][:, :, tt * 64:(tt + 1) * 64],
                                                                          in_=banks[k][:].rearrange("p (c s) -> p s c", s=8)),
                                        r=[("bank", k)], w=[("upsb", b, tt)])
                            if kind == "qk":
                                qr = [("qraw", b, tt) for tt in range(4)] + [("qraw", b, "pad")]
                                DVE(lambda e, b=b, cidx=cidx: e.tensor_scalar(out=acc[:], in0=qraw[b][:, 3:3 + S], scalar1=cw[:, cidx, 3:4],
                                                                              scalar2=cb[:, cidx:cidx + 1], op0=ALU.mult, op1=ALU.add),
                                    r=qr + [("cw", 0), ("cw", 1), ("cw", 2), ("cw", 3), "cb"], w=["cacc"])
                                for kx in range(3):
                                    DVE(lambda e, b=b, cidx=cidx, kx=kx: e.scalar_tensor_tensor(out=acc[:], in0=qraw[b][:, kx:kx + S],
                                                                                              scalar=cw[:, cidx, kx:kx + 1], in1=acc[:],
                                                                                              op0=ALU.mult, op1=ALU.add),
                                        r=qr + [("cw", 0), ("cw", 1), ("cw", 2), ("cw", 3), "cacc"], w=["cacc"])
                                ACT(lambda e, b=b: e.activation(out=qk_o[b][:], in_=acc[:], func=AF.Silu), r=["cacc"], w=[("qko", b)])
                                DMA("sp", T["qkT_d"][cidx * 128:(cidx + 1) * 128, :], qk_o[b][:], r=[("qko", b)])
                            else:
                                DMA("sp", T["uperm_d"][cidx * 128:(cidx + 1) * 128, :, :], up_sb[b][:], r=[("upsb", b, tt) for tt in range(4)])
                    else:
                        sidx = cnt.get(kind, 0)
                        if kind != "g":
                            cnt[kind] += 1
                        for i in range(16):
                            k = nbank()

                            def mm(e, sbuf=sbuf, i=i, k=k, ncol=ncol):
                                ins = None
                                for kt in range(16):
                                    ins = e.matmul(banks[k][:, 0:ncol], lhsT=hT[:, kt, i * 128:(i + 1) * 128], rhs=sbuf[:, kt, 0:ncol],
                                                   start=(kt == 0), stop=(kt == 15))
                                return ins
                            PE(mm, r=[skey] + hkeys, w=[("bank", k)])
                            b = cnt["ev"] % 2
                            cnt["ev"] += 1
                            if kind == "v":
                                DVE(lambda e, b=b, k=k: e.tensor_copy(out=v_sb[b][:, :, 0:256], in_=banks[k][:].rearrange("p (h v) -> p h v", h=2)),
                                    r=[("bank", k)], w=[("vsb", b)])
                                DMA("sp", T["V_d"][i * 128:(i + 1) * 128, sidx * 2:sidx * 2 + 2, :], v_sb[b][:], r=[("vsb", b), ("vone", b)])
                            elif kind == "o":
                                ACT(lambda e, b=b, k=k: e.activation(out=so_sb[b][:], in_=banks[k][:], func=AF.Sigmoid), r=[("bank", k)], w=[("sosb", b)])
                                DMA("sp", T["sigo_d"][i * 128:(i + 1) * 128, sidx * 512:(sidx + 1) * 512], so_sb[b][:], r=[("sosb", b)])
                            else:
                                ACT(lambda e, i=i, k=k: e.copy(out=gate_sb[:, i, :], in_=banks[k][:, 0:8]), r=[("bank", k)], w=[("gate", i)])
                DMA("sp", T["gate_d"].rearrange("(i p) c -> p i c", p=128), gate_sb[:], r=[("gate", i) for i in range(16)])
                P.barrier()

        def s5():
            import os
            stop = os.environ.get("S5_STOP", "")
            with ExitStack() as ps:
                Mall = SB(ps, "Mall", [128, 64, 128], BF16)
                GT = SB(ps, "GT", [128, 32, 2, 128], BF16)
                Pall = SB(ps, "Pall", [128, 32, 2, 128], BF16)
                with ExitStack() as pp:
                    lr = SB(pp, "lr", [128, 32])
                    li = SB(pp, "li", [128, 32])
                    ldt = SB(pp, "ldt", [128, 32])
                    Bre = SB(pp, "Bre", [128, 32, 16])
                    Bim = SB(pp, "Bim", [128, 32, 16])
                    CreT = SB(pp, "CreT", [128, 32, 16])
                    CimT = SB(pp, "CimT", [128, 32, 16])
                    dskb = SB(pp, "dskb", [128, 64])
                    for gb in range(2):
                        sl_ = slice(gb * 64, (gb + 1) * 64)
                        for nm, tl in (("lam_re", lr), ("lam_im", li)):
                            src = bass.AP(tensor=T[nm].tensor, offset=gb * 64, ap=[[1, 64], [128, 32]])
                            DMA("sp", tl[sl_, :], src, w=[nm], slow=True)
                        src = bass.AP(tensor=T["log_dt"].tensor, offset=gb, ap=[[0, 64], [2, 32]])
                        DMA("sp", ldt[sl_, :], src, w=["ldt"], slow=True)
                        for nm, tl in (("b_re", Bre), ("b_im", Bim)):
                            src = bass.AP(tensor=T[nm].tensor, offset=gb * 1024, ap=[[16, 64], [2048, 32], [1, 16]])
                            DMA("sp", tl[sl_, :, :], src, w=[nm])
                        for nm, tl in (("c_re", CreT), ("c_im", CimT)):
                            for gh_ in range(32):
                                src = bass.AP(tensor=T[nm].tensor, offset=gb * 1024 + gh_ * 2048, ap=[[1, 64], [64, 16]])
                                DMA("sp", tl[sl_, gh_, :], src, w=[(nm, gb, gh_)], slow=True)
                    for s_ in range(8):
                        src = bass.AP(tensor=T["d_skip"].tensor, offset=0, ap=[[1, 16], [16, 64]])
                        DMA("sp", dskb[s_ * 16:(s_ + 1) * 16, :], src, w=["dskb"], slow=True)

                    if stop == "dma":
                        P.barrier()
                        return
                    def tmp(name, shape, dt=F32):
                        return SB(pp, name, shape, dt)
                    dtt = tmp("dtt", [128, 32])
                    aa = tmp("aa", [128, 32])
                    th = tmp("th", [128, 32])
                    ka = tmp("ka", [128, 32, 9])
                    kth = tmp("kth", [128, 32, 9])
                    ek = tmp("ek", [128, 32, 9])
                    emk = tmp("emk", [128, 32, 9])
                    rs = tmp("rs", [128, 32, 9])
                    rc = tmp("rc", [128, 32, 9])
                    nint = tmp("nint", [128, 32, 9], I32)
                    nfl = tmp("nfl", [128, 32, 9])
                    sk = tmp("sk", [128, 32, 9])
                    ck = tmp("ck", [128, 32, 9])
                    pwr = tmp("pwr", [128, 32, 9])
                    pwi = tmp("pwi", [128, 32, 9])
                    ipr = tmp("ipr", [128, 32, 9])
                    ipi = tmp("ipi", [128, 32, 9])
                    t1 = tmp("t1", [128, 32])
                    t2 = tmp("t2", [128, 32])
                    nr = tmp("nr", [128, 32])
                    qre = tmp("qre", [128, 32])
                    qim = tmp("qim", [128, 32])
                    Bbr = tmp("Bbr", [128, 32, 16])
                    Bbi = tmp("Bbi", [128, 32, 16])
                    tb16 = tmp("tb16", [128, 32, 16])
                    E_re = tmp("E_re", [128, 32, 8, 16])
                    E_im = tmp("E_im", [128, 32, 8, 16])
                    F_re = tmp("F_re", [128, 32, 8, 16])
                    Fn_im = tmp("Fn_im", [128, 32, 8, 16])
                    G_re = tmp("G_re", [128, 32, 8, 16])
                    G_im = tmp("G_im", [128, 32, 8, 16])
                    t4 = tmp("t4", [128, 32, 8, 16])
                    SH4 = [128, 32, 8, 16]

                    def TT(out_, a, b, op, r, w):
                        DVE(lambda e: e.tensor_tensor(out=out_, in0=a, in1=b, op=op), r=r, w=w)

                    ACT(lambda e: e.activation(out=dtt[:], in_=ldt[:], func=AF.Exp), r=["ldt"], w=["dtt"])
                    TT(aa[:], lr[:], dtt[:], ALU.mult, ["lam_re", "dtt"], ["aa"])
                    TT(th[:], li[:], dtt[:], ALU.mult, ["lam_im", "dtt"], ["th"])
                    TT(ka[:], kk, aa[:].unsqueeze(2).broadcast_to([128, 32, 9]), ALU.mult, ["aa"], ["ka"])
                    TT(kth[:], kk, th[:].unsqueeze(2).broadcast_to([128, 32, 9]), ALU.mult, ["th"], ["kth"])
                    ACT(lambda e: e.activation(out=ek[:], in_=ka[:], func=AF.Exp), r=["ka"], w=["ek"])
                    ACT(lambda e: e.activation(out=emk[:], in_=ka[:], func=AF.Exp, scale=-1.0), r=["ka"], w=["emk"])
                    DVE(lambda e: e.tensor_scalar(out=rs[:], in0=kth[:], scalar1=1.0 / (2 * math.pi), scalar2=64.0, op0=ALU.mult, op1=ALU.add),
                        r=["kth"], w=["rs"])
                    DVE(lambda e: e.tensor_scalar(out=rc[:], in0=rs[:], scalar1=0.25, scalar2=1.0, op0=ALU.add, op1=ALU.mult), r=["rs"], w=["rc"])
                    for (rr, oo, nm) in ((rs, sk, "sk"), (rc, ck, "ck")):
                        DVE(lambda e, rr=rr: e.tensor_copy(out=nint[:], in_=rr[:]), r=["rs", "rc"], w=["nint"])
                        DVE(lambda e: e.tensor_copy(out=nfl[:], in_=nint[:]), r=["nint"], w=["nfl"])
                        TT(nfl[:], rr[:], nfl[:], ALU.subtract, ["nfl", "rs", "rc"], ["nfl"])
                        ACT(lambda e, oo=oo: e.activation(out=oo[:], in_=nfl[:], func=AF.Sin, scale=2 * math.pi), r=["nfl"], w=[nm])
                    TT(pwr[:], ek[:], ck[:], ALU.mult, ["ek", "ck"], ["pwr"])
                    TT(pwi[:], ek[:], sk[:], ALU.mult, ["ek", "sk"], ["pwi"])
                    TT(ipr[:], emk[:], ck[:], ALU.mult, ["emk", "ck"], ["ipr"])
                    DVE(lambda e: e.scalar_tensor_tensor(out=ipi[:], in0=emk[:], scalar=-1.0, in1=sk[:], op0=ALU.mult, op1=ALU.mult),
                        r=["emk", "sk"], w=["ipi"])
                    DVE(lambda e: e.tensor_copy(out=a8[:, :, 0], in_=pwr[:, :, 8]), r=["pwr"], w=["a8r"])
                    DVE(lambda e: e.tensor_copy(out=a8[:, :, 1], in_=pwi[:, :, 8]), r=["pwi"], w=["a8i"])
                    DVE(lambda e: e.tensor_scalar(out=nr[:], in0=pwr[:, :, 1], scalar1=-1.0, scalar2=1.0, op0=ALU.add, op1=ALU.mult), r=["pwr"], w=["nr"])
                    TT(t1[:], lr[:], lr[:], ALU.mult, ["lam_re"], ["t1"])
                    TT(t2[:], li[:], li[:], ALU.mult, ["lam_im"], ["t2"])
                    TT(t1[:], t1[:], t2[:], ALU.add, ["t1", "t2"], ["t1"])
                    DVE(lambda e: e.reciprocal(out=t1[:], in_=t1[:]), r=["t1"], w=["t1"])
                    TT(qre[:], nr[:], lr[:], ALU.mult, ["nr", "lam_re"], ["qre"])
                    TT(t2[:], pwi[:, :, 1], li[:], ALU.mult, ["pwi", "lam_im", "t2"], ["t2"])
                    TT(qre[:], qre[:], t2[:], ALU.add, ["qre", "t2"], ["qre"])
                    TT(qre[:], qre[:], t1[:], ALU.mult, ["qre", "t1"], ["qre"])
                    TT(qim[:], pwi[:, :, 1], lr[:], ALU.mult, ["pwi", "lam_re"], ["qim"])
                    TT(t2[:], nr[:], li[:], ALU.mult, ["nr", "lam_im", "t2"], ["t2"])
                    TT(qim[:], qim[:], t2[:], ALU.subtract, ["qim", "t2"], ["qim"])
                    TT(qim[:], qim[:], t1[:], ALU.mult, ["qim", "t1"], ["qim"])
                    b16 = lambda a: a.unsqueeze(2).broadcast_to([128, 32, 16])
                    TT(Bbr[:], Bre[:], b16(qre[:]), ALU.mult, ["b_re", "qre"], ["Bbr"])
                    TT(tb16[:], Bim[:], b16(qim[:]), ALU.mult, ["b_im", "qim"], ["tb16"])
                    TT(Bbr[:], Bbr[:], tb16[:], ALU.subtract, ["Bbr", "tb16"], ["Bbr"])
                    TT(Bbi[:], Bim[:], b16(qre[:]), ALU.mult, ["b_im", "qre"], ["Bbi"])
                    TT(tb16[:], Bre[:], b16(qim[:]), ALU.mult, ["b_re", "qim", "tb16"], ["tb16"])
                    TT(Bbi[:], Bbi[:], tb16[:], ALU.add, ["Bbi", "tb16"], ["Bbi"])
                    kb = lambda a: a[:, :, 0:8].unsqueeze(3).broadcast_to(SH4)
                    hb = lambda a: a.unsqueeze(2).broadcast_to(SH4)
                    gb4 = lambda a: a.unsqueeze(2).unsqueeze(3).broadcast_to(SH4)

                    def cmul(o_re, o_im, ar, ai, br, bi, rkeys, okeys, sub_re=True):
                        TT(o_re[:], ar, br, ALU.mult, rkeys, [okeys[0]])
                        TT(t4[:], ai, bi, ALU.mult, rkeys + ["t4"], ["t4"])
                        TT(o_re[:], o_re[:], t4[:], ALU.subtract, [okeys[0], "t4"], [okeys[0]])
                        TT(o_im[:], ar, bi, ALU.mult, rkeys, [okeys[1]])
                        TT(t4[:], ai, br, ALU.mult, rkeys + ["t4"], ["t4"])
                        TT(o_im[:], o_im[:], t4[:], ALU.add, [okeys[1], "t4"], [okeys[1]])
                    cmul(E_re, E_im, kb(ipr[:]), kb(ipi[:]), hb(Bbr[:]), hb(Bbi[:]), ["ipr", "ipi", "Bbr", "Bbi"], ["E_re", "E_im"])
                    cmul(F_re, Fn_im, kb(pwr[:]), kb(pwi[:]), hb(CreT[:]), hb(CimT[:]),
                         ["pwr", "pwi"] + [(nm_, gb_, gh_) for nm_ in ("c_re", "c_im") for gb_ in range(2) for gh_ in range(32)], ["F_re", "Fn_im"])
                    Pre_v = Pall[:, :, 0, :].rearrange("p g (j h) -> p g j h", h=16)
                    Pim_v = Pall[:, :, 1, :].rearrange("p g (j h) -> p g j h", h=16)
                    p1r = gb4(pwr[:, :, 1])
                    p1i = gb4(pwi[:, :, 1])
                    TT(G_re[:], F_re[:], p1r, ALU.mult, ["F_re", "pwr"], ["G_re"])
                    TT(t4[:], Fn_im[:], p1i, ALU.mult, ["Fn_im", "pwi"], ["t4"])
                    TT(G_re[:], G_re[:], t4[:], ALU.subtract, ["G_re", "t4"], ["G_re"])
                    DVE(lambda e: e.tensor_copy(out=Pre_v, in_=G_re[:]), r=["G_re"], w=["Pre"])
                    TT(G_re[:], F_re[:], p1i, ALU.mult, ["F_re", "pwi"], ["G_re"])
                    TT(t4[:], Fn_im[:], p1r, ALU.mult, ["Fn_im", "pwr"], ["t4"])
                    TT(G_re[:], G_re[:], t4[:], ALU.add, ["G_re", "t4"], ["G_re"])
                    DVE(lambda e: e.tensor_scalar(out=Pim_v, in0=G_re[:], scalar1=-1.0, scalar2=0.0, op0=ALU.mult, op1=ALU.add), r=["G_re"], w=["Pim"])
                    DVE(lambda e: e.tensor_scalar(out=Fn_im[:], in0=Fn_im[:], scalar1=-1.0, scalar2=0.0, op0=ALU.mult, op1=ALU.add), r=["Fn_im"], w=["Fn_im"])
                    cmul(G_re, G_im, gb4(pwr[:, :, 7]), gb4(pwi[:, :, 7]), E_re[:], E_im[:], ["pwr", "pwi", "E_re", "E_im"], ["G_re", "G_im"])
                    P.barrier()
                    if stop == "prep":
                        P.barrier()
                        return
                    mtmp = [SB(pp, f"mtmp{b}", [128, 4, 128]) for b in range(2)]
                    for g4 in range(16):
                        k = g4 % 2

                        def mmM(e, g4=g4, k=k):
                            ins = None
                            for j in range(4):
                                g = g4 * 4 + j
                                gh, gb = g // 2, g % 2
                                sl_ = slice(gb * 64, (gb + 1) * 64)
                                o = banks[k + 2 * gb][:, (j // 2) * 128:(j // 2 + 1) * 128]
                                e.matmul(o, lhsT=E_re[sl_, gh, :, :].rearrange("p s h -> p (s h)"),
                                         rhs=F_re[sl_, gh, :, :].rearrange("p s h -> p (s h)"), start=True, stop=False)
                                ins = e.matmul(o, lhsT=E_im[sl_, gh, :, :].rearrange("p s h -> p (s h)"),
                                               rhs=Fn_im[sl_, gh, :, :].rearrange("p s h -> p (s h)"), start=False, stop=True)
                            return ins
                        PE(mmM, w=[("bank", k), ("bank", k + 2)])
                        for gb in range(2):
                            DVE(lambda e, k=k, gb=gb: e.tensor_tensor(
                                out=mtmp[k][:].rearrange("p (jh gb) c -> p jh gb c", gb=2)[:, :, gb, :],
                                in0=banks[k + 2 * gb][:, 0:256].rearrange("p (j c) -> p j c", j=2),
                                in1=m5.unsqueeze(1).broadcast_to([128, 2, 128]), op=ALU.mult),
                                r=[("bank", k + 2 * gb)], w=[("mtmp", k, gb)])
                        for j in range(4):
                            g = g4 * 4 + j
                            DVE(lambda e, k=k, j=j, g=g: e.scalar_tensor_tensor(out=Mall[:, g, :], in0=ident, scalar=dskb[:, g:g + 1],
                                                                                in1=mtmp[k][:, j, :], op0=ALU.mult, op1=ALU.add),
                                r=[("mtmp", k, 0), ("mtmp", k, 1), "dskb"], w=[("Mall", g)])
                    for gh in range(32):
                        k = 4 + gh % 2

                        def trG(e, gh=gh, k=k):
                            e.matmul(banks[k][:, 0:128], lhsT=G_re[:, gh, :, :].rearrange("p s h -> p (s h)"), rhs=ident, start=True, stop=True)
                            return e.matmul(banks[k][:, 128:256], lhsT=G_im[:, gh, :, :].rearrange("p s h -> p (s h)"), rhs=ident, start=True, stop=True)
                        PE(trG, w=[("bank", k)])
                        ACT(lambda e, gh=gh, k=k: e.copy(out=GT[:, gh, :, :], in_=banks[k][:, 0:256].rearrange("p (r c) -> p r c", r=2)),
                            r=[("bank", k)], w=[("GT", gh)])
                    P.barrier()
                if stop == "mm":
                    return
                with ExitStack() as pr:
                    X = SB(pr, "X5", [128, 32, 2, 256])
                    Xp = SB(pr, "Xp5", [128, 32, 2, 256], BF16)
                    Uall = SB(pr, "Uall", [128, 64, 256], BF16)
                    T1 = SB(pr, "T1s", [128, 32, 2])
                    T2 = SB(pr, "T2s", [128, 32, 2])
                    ysb = [SB(pr, f"ysb{b}", [128, 512]) for b in range(2)]
                    ytm = [SB(pr, f"ytm{b}", [128, 512]) for b in range(2)]
                    for s_ in range(8):
                        DMA("sp", Uall[s_ * 16:(s_ + 1) * 16, :, :], T["uperm_d"][:, s_, :].rearrange("(g h) c -> h g c", h=16), w=[("U", s_)])
                    ukeys = [("U", s_) for s_ in range(8)]
                    for gh in range(32):
                        k = gh % 2

                        def mmZ(e, gh=gh, k=k):
                            ins = None
                            for gb in range(2):
                                for ri in range(2):
                                    ins = e.matmul(banks[k][gb * 64:(gb + 1) * 64, ri * 256:(ri + 1) * 256],
                                                   lhsT=GT[:, gh, ri, gb * 64:(gb + 1) * 64], rhs=Uall[:, 2 * gh + gb, :], start=True, stop=True)
                            return ins
                        PE(mmZ, r=ukeys, w=[("bank", k)])
                        src = banks[k][:].rearrange("p (r c) -> p r c", r=2)
                        if gh % 2 == 0:
                            ACT(lambda e, gh=gh, src=src: e.copy(out=X[:, gh, :, :], in_=src), r=[("bank", k)], w=["X"])
                        else:
                            DVE(lambda e, gh=gh, src=src: e.tensor_copy(out=X[:, gh, :, :], in_=src), r=[("bank", k)], w=["X2"])
                    if stop == "z":
                        P.barrier()
                        return
                    a8r_b = a8[:, :, 0:1].broadcast_to([128, 32, 2])
                    a8i_b = a8[:, :, 1:2].broadcast_to([128, 32, 2])
                    for c in range(1, 256):
                        prev = X[:, :, :, c - 1]
                        cur = X[:, :, :, c]
                        DVE(lambda e, prev=prev: e.tensor_tensor(out=T1[:], in0=prev, in1=a8r_b, op=ALU.mult), r=["X", "X2"], w=["T1"])
                        DVE(lambda e, prev=prev: e.tensor_tensor(out=T2[:], in0=prev, in1=a8i_b, op=ALU.mult), r=["X", "X2"], w=["T2"])
                        DVE(lambda e, cur=cur: e.tensor_tensor(out=cur, in0=cur, in1=T1[:], op=ALU.add), r=["T1", "X2"], w=["X"])
                        DVE(lambda e, c=c: e.tensor_tensor(out=X[:, :, 0, c], in0=X[:, :, 0, c], in1=T2[:, :, 1], op=ALU.subtract), r=["T2", "X"], w=["X"])
                        DVE(lambda e, c=c: e.tensor_tensor(out=X[:, :, 1, c], in0=X[:, :, 1, c], in1=T2[:, :, 0], op=ALU.add), r=["T2", "X"], w=["X"])
                    if stop == "scan":
                        P.barrier()
                        return
                    DVE(lambda e: e.memset(Xp[:, :, :, 0:1], 0.0), w=["Xp0"])
                    ACT(lambda e: e.copy(out=Xp[:, :, :, 1:256], in_=X[:, :, :, 0:255]), r=["X", "X2"], w=["Xp"])
                    for g2 in range(32):
                        k = g2 % 2

                        def mmY(e, g2=g2, k=k):
                            ins = None
                            for gb in range(2):
                                g = g2 * 2 + gb
                                sl_ = slice(gb * 64, (gb + 1) * 64)
                                o = banks[k][:, gb * 256:(gb + 1) * 256]
                                e.matmul(o, lhsT=Mall[:, g, :], rhs=Uall[:, g, :], start=True, stop=False)
                                e.matmul(o, lhsT=Pall[sl_, g2, 0, :], rhs=Xp[sl_, g2, 0, :], start=False, stop=False)
                                ins = e.matmul(o, lhsT=Pall[sl_, g2, 1, :], rhs=Xp[sl_, g2, 1, :], start=False, stop=True)
                            return ins
                        PE(mmY, r=ukeys + ["Xp", "Xp0"], w=[("bank", k)])
                        ACT(lambda e, k=k: e.copy(out=ysb[k][:], in_=banks[k][:]), r=[("bank", k)], w=[("ysb", k)])
                        DVE(lambda e, k=k: e.tensor_tensor(out=ytm[k][:], in0=ysb[k][:], in1=ysb[k][:], op=ALU.mult), r=[("ysb", k)], w=[("ytm", k)])
                        DVE(lambda e, k=k: e.tensor_scalar(out=ytm[k][:], in0=ytm[k][:], scalar1=0.044715, scalar2=1.0, op0=ALU.mult, op1=ALU.add),
                            r=[("ytm", k)], w=[("ytm", k)])
                        DVE(lambda e, k=k: e.tensor_tensor(out=ytm[k][:], in0=ytm[k][:], in1=ysb[k][:], op=ALU.mult), r=[("ytm", k), ("ysb", k)], w=[("ytm", k)])
                        ACT(lambda e, k=k: e.activation(out=ytm[k][:], in_=ytm[k][:], func=AF.Sigmoid, scale=2.0 * math.sqrt(2.0 / math.pi)),
                            r=[("ytm", k)], w=[("ytm", k)])
                        DVE(lambda e, k=k, g2=g2: e.tensor_tensor(out=Uall[:, 2 * g2:2 * g2 + 2, :], in0=ysb[k][:].rearrange("p (g c) -> p g c", g=2),
                                                                  in1=ytm[k][:].rearrange("p (g c) -> p g c", g=2), op=ALU.mult),
                            r=[("ytm", k), ("ysb", k)] + ukeys, w=[("Y", g2)])
                    ykeys = [("Y", g2) for g2 in range(32)]
                    for j in range(8):
                        DMA("sp", T["yperm_d"][:, j, :].rearrange("(g h) c -> h g c", h=16), Uall[j * 16:(j + 1) * 16, :, :], r=ykeys)
                    P.barrier()

        def glu():
            with ExitStack() as ps:
                ys = SB(ps, "ysg", [128, 8, S], BF16)
                wg = [SB(ps, f"wg{b}", [128, 8, 128], BF16) for b in range(2)]
                bg = SB(ps, "bglu", [128, 8])
                sg = [SB(ps, f"sgg{b}", [128, 512]) for b in range(2)]
                hs_o = [SB(ps, f"hso{b}", [128, S], BF16) for b in range(2)]
                DMA("sp", ys[:], T["yperm_d"].rearrange("(k p) j c -> p k (j c)", p=128), w=["ys"])
                DMA("sp", bg[:], T["b_glu"].rearrange("o (c p) -> p (o c)", p=128), w=["bg"], slow=True)
                cnt = 0
                for n in range(8):
                    b = n % 2
                    DMA("pool", wg[b][:], T["w_glu"][:, n * 128:(n + 1) * 128].rearrange("(k p) n -> p k n", p=128), w=[("wg", b)])
                    for tt in range(4):
                        k = cnt % 2
                        cnt += 1

                        def mm(e, b=b, tt=tt, k=k):
                            ins = None
                            for kc in range(8):
                                ins = e.matmul(banks[k][:], lhsT=wg[b][:, kc, :], rhs=ys[:, kc, tt * 512:(tt + 1) * 512], start=(kc == 0), stop=(kc == 7))
                            return ins
                        PE(mm, r=[("wg", b), "ys"], w=[("bank", k)])
                        ACT(lambda e, k=k, n=n: e.activation(out=sg[k][:], in_=banks[k][:], func=AF.Sigmoid, bias=bg[:, n:n + 1], scale=1.0),
                            r=[("bank", k), "bg"], w=[("sg", k)])
                        DVE(lambda e, k=k, n=n, tt=tt, b=b: e.tensor_tensor(
                            out=hs_o[b][:].rearrange("p (c j) -> p j c", j=8)[:, 2 * tt:2 * tt + 2, :],
                            in0=ys[:, n, tt * 512:(tt + 1) * 512].rearrange("p (j c) -> p j c", j=2),
                            in1=sg[k][:].rearrange("p (j c) -> p j c", j=2), op=ALU.mult),
                            r=[("sg", k), "ys"], w=[("hso", b, tt)])
                    DMA("sp", T["hsT_d"][n * 128:(n + 1) * 128, :], hs_o[b][:], r=[("hso", b, tt) for tt in range(4)])
                P.barrier()

        def mlstm():
            with ExitStack() as ps:
                qk = SB(ps, "qkm", [128, 8, S], BF16)
                V = SB(ps, "Vm", [128, 16, 4, 257], BF16)
                Ktm = SB(ps, "Ktm", [128, 16, 4, 128], BF16)
                gt = SB(ps, "gtm", [128, 16, 8])
                bif = SB(ps, "bif", [128, 8])
                gn = SB(ps, "gnm", [128, 1024])
                ipre = SB(ps, "ipre", [128, 16, 4])
                l1 = SB(ps, "l1", [128, 16, 4])
                wv = SB(ps, "wv", [128, 16, 4])
                ebs = SB(ps, "ebs", [128, 16, 4])
                ebend = SB(ps, "ebend", [128, 16, 4])
                Cf = [SB(ps, f"Cf{h}", [128, 257]) for h in range(4)]
                Cb = [SB(ps, f"Cb{h}", [128, 257], BF16) for h in range(4)]
                STm = [SB(ps, f"STm{b}", [128, 128], BF16) for b in range(2)]
                hsb = [SB(ps, f"hsb{b}", [128, 256]) for b in range(2)]
                junk = SB(ps, "junkm", [128, 256])
                tiny = SB(ps, "tinym", [128, 64, 4])
                DVE(lambda e: e.memset(tiny[:], 0.0), w=["tiny0"])
                hm_tm = [SB(ps, f"hmtm{b}", [128, 1024], BF16) for b in range(2)]
                sg = [SB(ps, f"sgo{b}", [128, 1024]) for b in range(2)]
                hmT_o = [SB(ps, f"hmTo{b}", [128, 8, 128], BF16) for b in range(2)]
                DMA("sp", qk[:], T["qkT_d"].rearrange("(c p) t -> p c t", p=128), w=["qk"])
                DMA("sp", V[:], T["V_d"].rearrange("(i p) h v -> p i h v", p=128), w=["V"])
                DMA("sp", gt[:], T["gate_d"].rearrange("(i p) c -> p i c", p=128), w=["gt"])
                DMA("sp", bif[:, 0:4], bcast_rows(T["b_i"], 4), w=["bif0"])
                DMA("sp", bif[:, 4:8], bcast_rows(T["b_f"], 4), w=["bif1"])
                DMA("sp", gn[:], bcast_rows(T["mlstm_norm"], 1024), w=["gn"])
                b4 = lambda a: a.unsqueeze(1).broadcast_to([128, 16, 4])
                DVE(lambda e: e.tensor_tensor(out=ipre[:], in0=gt[:, :, 0:4], in1=b4(bif[:, 0:4]), op=ALU.add), r=["gt", "bif0"], w=["ipre"])
                DVE(lambda e: e.tensor_tensor(out=l1[:], in0=gt[:, :, 4:8], in1=b4(bif[:, 4:8]), op=ALU.add), r=["gt", "bif1"], w=["l1"])
                ACT(lambda e: e.activation(out=l1[:], in_=l1[:], func=AF.Exp, scale=-1.0), r=["l1"], w=["l1"])
                ACT(lambda e: e.activation(out=l1[:], in_=l1[:], func=AF.Ln, bias=1.0, scale=1.0), r=["l1"], w=["l1"])
                l1f = l1[:].rearrange("p i h -> p (i h)")

                def mmg(e):
                    e.matmul(banks[7][:, 0:64], lhsT=tri, rhs=l1f, start=True, stop=True)
                    return e.matmul(banks[7][:, 64:128], lhsT=ones, rhs=l1f, start=True, stop=True)
                PE(mmg, r=["l1"], w=[("bank", 7)])
                bpos = banks[7][:, 0:64].rearrange("p (i h) -> p i h", h=4)
                bendpos = banks[7][:, 64:128].rearrange("p (i h) -> p i h", h=4)
                DVE(lambda e: e.tensor_tensor(out=wv[:], in0=ipre[:], in1=bpos, op=ALU.add), r=["ipre", ("bank", 7)], w=["wv"])
                ACT(lambda e: e.activation(out=wv[:], in_=wv[:], func=AF.Exp), r=["wv"], w=["wv"])
                ACT(lambda e: e.activation(out=ebs[:], in_=bpos, func=AF.Exp, scale=-1.0, bias=-0.5 * math.log(128.0)), r=[("bank", 7)], w=["ebs"])
                ACT(lambda e: e.activation(out=ebend[:], in_=bendpos, func=AF.Exp, scale=-1.0), r=[("bank", 7)], w=["ebend"])
                for i in range(16):
                    DVE(lambda e, i=i: e.tensor_tensor(out=V[:, i, :, :], in0=V[:, i, :, :], in1=wv[:, i, :].unsqueeze(2).broadcast_to([128, 4, 257]),
                                                       op=ALU.mult), r=["V", "wv"], w=[("Vw", i)])
                    k = 6

                    def trK(e, i=i):
                        ins = None
                        for h in range(4):
                            ins = e.matmul(banks[6][:, h * 128:(h + 1) * 128], lhsT=qk[:, 4 + h, i * 128:(i + 1) * 128], rhs=ident_bf[:], start=True, stop=True)
                        return ins
                    PE(trK, r=["qk"], w=[("bank", 6)])
                    ACT(lambda e, i=i: e.copy(out=Ktm[:, i, :, :], in_=banks[6][:].rearrange("p (h d) -> p h d", h=4)),
                        r=[("bank", 6)], w=[("Ktm", i)])
                cnt = 0
                for i in range(16):
                    bi = i % 2
                    DMA("sp", sg[bi][:], T["sigo_d"][i * 128:(i + 1) * 128, :], w=[("sgo", bi)])
                    for h in range(4):
                        ka_ = cnt % 2
                        kc_ = 2 + cnt % 2
                        kd_ = 4 + cnt % 2
                        sb_ = cnt % 2
                        col = i * 4 + h
                        cnt += 1
                        tsl = slice(i * 128, (i + 1) * 128)
                        PE(lambda e, h=h, tsl=tsl, ka_=ka_: e.matmul(banks[ka_][:, 0:128], lhsT=qk[:, 4 + h, tsl], rhs=qk[:, h, tsl], start=True, stop=True),
                           r=["qk"], w=[("bank", ka_)])
                        DVE(lambda e, ka_=ka_, sb_=sb_: e.tensor_tensor(out=STm[sb_][:], in0=banks[ka_][:, 0:128], in1=tri, op=ALU.mult),
                            r=[("bank", ka_)], w=[("STm", sb_)])

                        def mmN(e, i=i, h=h, tsl=tsl, kc_=kc_, sb_=sb_):
                            ins = e.matmul(banks[kc_][:, 0:257], lhsT=STm[sb_][:], rhs=V[:, i, h, :], start=True, stop=(i == 0))
                            if i > 0:
                                ins = e.matmul(banks[kc_][:, 0:257], lhsT=qk[:, h, tsl], rhs=Cb[h][:], start=False, stop=True)
                            return ins
                        PE(mmN, r=[("STm", sb_), ("Vw", i), "qk", ("Cb", h)], w=[("bank", kc_)])
                        if i < 15:
                            PE(lambda e, i=i, h=h, kd_=kd_: e.matmul(banks[kd_][:, 0:257], lhsT=Ktm[:, i, h, :], rhs=V[:, i, h, :], start=True, stop=True),
                               r=[("Ktm", i), ("Vw", i)], w=[("bank", kd_)])
                            ecol = ebend[:, i, h:h + 1]
                            if i == 0:
                                DVE(lambda e, h=h, kd_=kd_, ecol=ecol: e.tensor_scalar(out=Cf[h][:], in0=banks[kd_][:, 0:257], scalar1=ecol, scalar2=0.0, op0=ALU.mult, op1=ALU.add),
                                    r=[("bank", kd_), "ebend"], w=[("Cf", h)])
                            else:
                                DVE(lambda e, h=h, ecol=ecol: e.tensor_scalar(out=Cf[h][:], in0=Cf[h][:], scalar1=ecol, scalar2=0.0, op0=ALU.mult, op1=ALU.add),
                                    r=[("Cf", h), "ebend"], w=[("Cf", h)])
                                DVE(lambda e, h=h, kd_=kd_, ecol=ecol: e.scalar_tensor_tensor(out=Cf[h][:], in0=banks[kd_][:, 0:257], scalar=ecol, in1=Cf[h][:],
                                                                                              op0=ALU.mult, op1=ALU.add),
                                    r=[("bank", kd_), ("Cf", h), "ebend"], w=[("Cf", h)])
                            ACT(lambda e, h=h: e.copy(out=Cb[h][:], in_=Cf[h][:]), r=[("Cf", h)], w=[("Cb", h)])
                        tk = ("tiny", col)
                        ACT(lambda e, kc_=kc_, col=col: e.activation(out=tiny[:, col, 0:1], in_=banks[kc_][:, 256:257], func=AF.Abs), r=[("bank", kc_), "tiny0"], w=[tk])
                        DVE(lambda e, col=col, i=i, h=h: e.tensor_scalar(out=tiny[:, col, 1:2], in0=tiny[:, col, 0:1], scalar1=ebs[:, i, h:h + 1], scalar2=1.0,
                                                                         op0=ALU.mult, op1=ALU.max), r=[tk, "ebs"], w=[tk])
                        DVE(lambda e, col=col: e.reciprocal(out=tiny[:, col, 1:2], in_=tiny[:, col, 1:2]), r=[tk], w=[tk])
                        DVE(lambda e, col=col, i=i, h=h: e.tensor_tensor(out=tiny[:, col, 2:3], in0=tiny[:, col, 1:2], in1=ebs[:, i, h:h + 1], op=ALU.mult),
                            r=[tk, "ebs"], w=[tk])
                        DVE(lambda e, kc_=kc_, col=col, sb_=sb_: e.tensor_scalar(out=hsb[sb_][:], in0=banks[kc_][:, 0:256], scalar1=tiny[:, col, 2:3], scalar2=0.0,
                                                                                 op0=ALU.mult, op1=ALU.add),
                            r=[("bank", kc_), tk], w=[("hsb", sb_)])
                        ACT(lambda e, col=col, sb_=sb_: e.activation(out=junk[:], in_=hsb[sb_][:], func=AF.Square, accum_out=tiny[:, col, 3:4]),
                            r=[("hsb", sb_)], w=[tk, "junk"])
                        ACT(lambda e, col=col: e.activation(out=tiny[:, col, 3:4], in_=tiny[:, col, 3:4], func=AF.Ln, scale=1.0 / 256, bias=EPS), r=[tk], w=[tk])
                        ACT(lambda e, col=col: e.activation(out=tiny[:, col, 3:4], in_=tiny[:, col, 3:4], func=AF.Exp, scale=-0.5), r=[tk], w=[tk])
                        DVE(lambda e, col=col, sb_=sb_, h=h: e.scalar_tensor_tensor(out=hsb[sb_][:], in0=hsb[sb_][:], scalar=tiny[:, col, 3:4],
                                                                                   in1=gn[:, h * 256:(h + 1) * 256], op0=ALU.mult, op1=ALU.mult),
                            r=[("hsb", sb_), tk, "gn"], w=[("hsb", sb_)])
                        DVE(lambda e, sb_=sb_, h=h, bi=bi: e.tensor_tensor(out=hm_tm[bi][:, h * 256:(h + 1) * 256], in0=hsb[sb_][:],
                                                                         in1=sg[bi][:, h * 256:(h + 1) * 256], op=ALU.mult),
                            r=[("hsb", sb_), ("sgo", bi)], w=[("hmtm", bi, h)])

                    def trH(e, bi=bi):
                        ins = None
                        for c in range(8):
                            bk = banks[7] if c < 4 else banks[6]
                            ins = e.matmul(bk[:, (c % 4) * 128:(c % 4 + 1) * 128], lhsT=hm_tm[bi][:, c * 128:(c + 1) * 128], rhs=ident_bf[:], start=True, stop=True)
                        return ins
                    PE(trH, r=[("hmtm", bi, h) for h in range(4)], w=[("bank", 7), ("bank", 6)])
                    ACT(lambda e, bi=bi: e.copy(out=hmT_o[bi][:, 0:4, :], in_=banks[7][:].rearrange("p (c t) -> p c t", c=4)),
                        r=[("bank", 7)], w=[("hmTo", bi, 0)])
                    ACT(lambda e, bi=bi: e.copy(out=hmT_o[bi][:, 4:8, :], in_=banks[6][:].rearrange("p (c t) -> p c t", c=4)),
                        r=[("bank", 6)], w=[("hmTo", bi, 1)])
                    DMA("sp", T["hmT_d"].rearrange("(c p) t -> p c t", p=128)[:, :, i * 128:(i + 1) * 128], hmT_o[bi][:], r=[("hmTo", bi, 0), ("hmTo", bi, 1)])
                P.barrier()

        def wout():
            with ExitStack() as ps:
                act = SB(ps, "actT", [128, 16, S], BF16)
                wo = [SB(ps, f"wo{b}", [128, 16, 128], BF16) for b in range(2)]
                xr = [SB(ps, f"xrw{b}", [128, 512]) for b in range(2)]
                xo = [SB(ps, f"xow{b}", [128, 512]) for b in range(2)]
                DMA("sp", act[:, 0:8, :], T["hmT_d"].rearrange("(c p) t -> p c t", p=128), w=["act0"])
                DMA("sp", act[:, 8:16, :], T["hsT_d"].rearrange("(c p) t -> p c t", p=128), w=["act1"])
                cnt = 0
                for dt in range(16):
                    b = dt % 2
                    DMA("pool", wo[b][:], T["w_out"][:, dt * 128:(dt + 1) * 128].rearrange("(k p) n -> p k n", p=128), w=[("wo", b)])
                    for tt in range(4):
                        k = cnt % 2
                        cnt += 1
                        DMA("sp", xr[k][:], T["xT1"][dt * 128:(dt + 1) * 128, tt * 512:(tt + 1) * 512], w=[("xr", k)])

                        def mm(e, b=b, tt=tt, k=k):
                            ins = None
                            for kc in range(16):
                                ins = e.matmul(banks[k][:], lhsT=wo[b][:, kc, :], rhs=act[:, kc, tt * 512:(tt + 1) * 512], start=(kc == 0), stop=(kc == 15))
                            return ins
                        PE(mm, r=[("wo", b), "act0", "act1"], w=[("bank", k)])
                        DVE(lambda e, k=k: e.tensor_tensor(out=xo[k][:], in0=banks[k][:], in1=xr[k][:], op=ALU.add), r=[("bank", k), ("xr", k)], w=[("xo", k)])
                        DMA("sp", T["xT2"][dt * 128:(dt + 1) * 128, tt * 512:(tt + 1) * 512], xo[k][:], r=[("xo", k)])
                P.barrier()

        def final():
            with ExitStack() as ps:
                gcol = load_gcol(ps, "gcolf", T["final_norm"])
                yn = [SB(ps, f"ynf{b}", [128, 512]) for b in range(2)]
                otm = SB(ps, "otm", [128, 4, D])
                outv = out.rearrange("(n p) d -> p n d", p=128)
                state = {"cnt": 0}

                def emit2(tb, dt, xtile, xkey, rstd, g):
                    b = dt % 2
                    k = state["cnt"] % 4
                    state["cnt"] += 1
                    DVE(lambda e: e.scalar_tensor_tensor(out=yn[b][:], in0=xtile[:], scalar=g[:, dt:dt + 1], in1=rstd[:], op0=ALU.mult, op1=ALU.mult),
                        r=[xkey, "rstd", "gcolf"], w=[("yn", b)])

                    def tr(e):
                        ins = None
                        for j in range(4):
                            ins = e.matmul(banks[k][:, j * 128:(j + 1) * 128], lhsT=yn[b][:, j * 128:(j + 1) * 128], rhs=ident, start=True, stop=True)
                        return ins
                    PE(tr, r=[("yn", b)], w=[("bank", k)])
                    ACT(lambda e: e.copy(out=otm[:, :, dt * 128:(dt + 1) * 128], in_=banks[k][:].rearrange("p (j d) -> p j d", j=4)),
                        r=[("bank", k)], w=[("otm", dt)])
                    if dt == 15:
                        DMA("sp", outv[:, tb * 4:(tb + 1) * 4, :], otm[:], r=[("otm", d_) for d_ in range(16)])
                norm_fm(ps, T["xT3"], 0, 4, gcol, emit2, "f")()
                P.barrier()

        if "p0" in phases:
            phase0()
        if "ffn1" in phases:
            ffn(T["xT0"], T["xT1"], T["ffn1_norm"], T["ffn1_w1"], T["ffn1_w3"], T["ffn1_w2"], "a")
        if "proj" in phases:
            proj()
        if "s5" in phases:
            s5()
        if "glu" in phases:
            glu()
        if "mlstm" in phases:
            mlstm()
        if "wout" in phases:
            wout()
        if "ffn2" in phases:
            ffn(T["xT2"], T["xT3"], T["ffn2_norm"], T["ffn2_w1"], T["ffn2_w3"], T["ffn2_w2"], "b")
        if "final" in phases:
            final()
        P.emit(st)
        build_nc.last_prog = P
    return nc


def core_inputs(inputs, b):
    m = {}
    for k, shp in PARAM_SHAPES.items():
        if k == "consts":
            m[k] = make_consts()
        elif k == "x":
            m[k] = np.ascontiguousarray(inputs["x"][b], dtype=np.float32)
        else:
            m[k] = np.ascontiguousarray(np.asarray(inputs[k], dtype=np.float32).reshape(shp))
    return m


_NC_CACHE = {}


def kernel(**inputs):
    if "nc" not in _NC_CACHE:
        _NC_CACHE["nc"] = build_nc()
    nc = _NC_CACHE["nc"]
    in_maps = [core_inputs(inputs, b) for b in range(8)]
    res = run_bass_kernel_spmd(nc, in_maps, core_ids=list(range(8)))
    return np.stack([np.asarray(r["out"], dtype=np.float32) for r in res.results], axis=0)
```
